# Optimizing a Trainium2 kernel written in Bass

```python
import math
import jax, jax.numpy as jnp
from jax import lax
import numpy as np

D_MODEL = 2048
BATCH = 1
SEQ = 8192
DEPTH = 2
DEC_BATCH = 16
DEC_SEQ = 2048
PAST_LEN = 128

PLE_DIM = 256
GRID_W = 64
QBLK = 128
N_BRANCH = 4
H_A = 4
DK_A = 64
DV_A = 128
H_B = 4
Q_RANK_B = 512
KV_RANK_B = 512
DN_B = 128
DR_B = 64
DV_B = 128
H_C = 4
KV_C = 2
DH_C = 128
H_D = 4
DH_D = 128
DILATIONS = ((128, 1), (512, 4), (2048, 16))
N_DIL = 3
BRANCH_W = 512
D_FF = 4 * D_MODEL
N_BUCKETS = 32
REL_MAX_DIST = 1024
N_BIAS_HEADS = H_A + N_DIL * H_D
ROPE_THETA = 10000.0
NORM_EPS = 1e-6
NEG_INF = -1e30
IN_SIZES = (H_A * 2 * DK_A, H_A * 2 * DK_A, H_A * DV_A,
            Q_RANK_B, KV_RANK_B, DR_B,
            H_C * DH_C, KV_C * DH_C, KV_C * DH_C,
            N_DIL * H_D * DH_D, H_D * DH_D, H_D * DH_D)
IN_SPLITS = tuple(int(s) for s in np.cumsum(IN_SIZES)[:-1])
IN_WIDTH = int(sum(IN_SIZES))

kernel_name = 'gated_hybrid_bidir_encoder'


def rms_norm(x, g):
    xf = x.astype(jnp.float32)
    y = xf * lax.rsqrt(jnp.mean(xf * xf, axis=-1, keepdims=True) + NORM_EPS)
    return (y * g).astype(x.dtype)


def rel_bucket(rel):
    half = N_BUCKETS // 2
    exact = half // 2
    n = jnp.abs(rel)
    nf = jnp.maximum(n, 1).astype(jnp.float32)
    large = exact + (jnp.log(nf / exact) / math.log(REL_MAX_DIST / exact) * (half - exact)).astype(jnp.int32)
    large = jnp.minimum(large, half - 1)
    return jnp.where(rel > 0, half, 0) + jnp.where(n < exact, n, large)


def rope(x, pos):
    d = x.shape[-1]
    half = d // 2
    inv = jnp.power(ROPE_THETA, -2.0 * jnp.arange(half, dtype=jnp.float32) / d)
    ang = pos[:, None] * inv[None, :]
    cos = jnp.cos(ang)[:, None, :]
    sin = jnp.sin(ang)[:, None, :]
    x1 = x[..., :half].astype(jnp.float32)
    x2 = x[..., half:].astype(jnp.float32)
    return jnp.concatenate([x1 * cos - x2 * sin, x2 * cos + x1 * sin], axis=-1).astype(x.dtype)


def axial_rope(x, row, col):
    half = x.shape[-1] // 2
    return jnp.concatenate([rope(x[..., :half], row), rope(x[..., half:], col)], axis=-1)


def dense_attention(q, k, v, bias_tbl, scale):
    B, S, Hk, G, dk = q.shape
    T = k.shape[1]
    nb = S // QBLK
    qb = q.reshape(B, nb, QBLK, Hk, G, dk).transpose(1, 0, 2, 3, 4, 5)
    kpos = jnp.arange(T, dtype=jnp.int32)

    def one(args):
        qblk, idx = args
        s = jnp.einsum('bqhgd,bkhd->bhgqk', qblk, k, preferred_element_type=jnp.float32) * scale
        if bias_tbl is not None:
            qpos = idx * QBLK + jnp.arange(QBLK, dtype=jnp.int32)
            bucket = rel_bucket(kpos[None, :] - qpos[:, None])
            s = s + bias_tbl[bucket].transpose(2, 3, 0, 1)
        p = jax.nn.softmax(s, axis=-1)
        return jnp.einsum('bhgqk,bkhd->bqhgd', p.astype(v.dtype), v)

    o = lax.map(one, (qb, jnp.arange(nb, dtype=jnp.int32)))
    return o.transpose(1, 0, 2, 3, 4, 5).reshape(B, S, Hk, G, v.shape[-1])


def dilated_band_attention(q, k, v, window, dil, bias_tbl, scale):
    B, S, H, d = q.shape
    half = window // (2 * dil)
    L = S // dil
    nb = -(-L // half)
    Lp = nb * half

    def to_sub(x):
        return x.reshape(B, L, dil, H, x.shape[-1]).transpose(0, 2, 1, 3, 4)

    qs = jnp.pad(to_sub(q), ((0, 0), (0, 0), (0, Lp - L), (0, 0), (0, 0))).reshape(B, dil, nb, half, H, d)

    def key_blocks(x):
        xp = jnp.pad(to_sub(x), ((0, 0), (0, 0), (half, Lp - L + half), (0, 0), (0, 0)))
        xb = xp.reshape(B, dil, nb + 2, half, H, x.shape[-1])
        return jnp.concatenate([xb[:, :, :-2], xb[:, :, 1:-1], xb[:, :, 2:]], axis=3)

    kb = key_blocks(k)
    vb = key_blocks(v)
    s = jnp.einsum('brnqhd,brnkhd->brnhqk', qs, kb, preferred_element_type=jnp.float32) * scale
    qi = jnp.arange(half, dtype=jnp.int32)[:, None]
    kj = jnp.arange(3 * half, dtype=jnp.int32)[None, :]
    rel = kj - half - qi
    bias = bias_tbl[rel_bucket(rel * dil)].transpose(2, 0, 1)
    kpos = (jnp.arange(nb, dtype=jnp.int32)[:, None] - 1) * half + kj
    valid = (kpos >= 0) & (kpos < L)
    mask = (jnp.abs(rel) <= half)[None] & valid[:, None, :]
    s = jnp.where(mask[:, None], s + bias, NEG_INF)
    lse = jax.nn.logsumexp(s, axis=-1)
    p = jnp.exp(s - lse[..., None])
    o = jnp.einsum('brnhqk,brnkhd->brnqhd', p.astype(v.dtype), vb)
    o = o.reshape(B, dil, Lp, H, d)[:, :, :L].transpose(0, 2, 1, 3, 4).reshape(B, S, H, d)
    lse = lse.transpose(0, 1, 2, 4, 3).reshape(B, dil, Lp, H)[:, :, :L].transpose(0, 2, 1, 3).reshape(B, S, H)
    return o, lse


def diff_attention(aq, ak, av, q_g, k_g, lq1, lk1, lq2, lk2, out_g, bias_tbl, lam_init):
    B, S, _ = aq.shape
    q = rms_norm(aq.reshape(B, S, H_A, 2, DK_A), q_g)
    k = rms_norm(ak.reshape(B, S, H_A, 2, DK_A), k_g)
    v = av.reshape(B, S, H_A, DV_A)
    tbl = bias_tbl[:, :, None]
    o1 = dense_attention(q[:, :, :, 0, None], k[:, :, :, 0], v, tbl, DK_A ** -0.5)
    o2 = dense_attention(q[:, :, :, 1, None], k[:, :, :, 1], v, tbl, DK_A ** -0.5)
    lam = (jnp.exp(jnp.sum(lq1.astype(jnp.float32) * lk1.astype(jnp.float32)))
           - jnp.exp(jnp.sum(lq2.astype(jnp.float32) * lk2.astype(jnp.float32))) + lam_init)
    o = (o1 - lam.astype(o1.dtype) * o2)[:, :, :, 0]
    o = rms_norm(o, out_g) * (1.0 - lam_init)
    return o.reshape(B, S, H_A * DV_A)


def latent_attention(cq, ckv, kr, cq_g, ckv_g, w_uq, w_ukv, q_g, k_g, pos):
    B, S, _ = cq.shape
    q = (rms_norm(cq, cq_g) @ w_uq).reshape(B, S, H_B, DN_B + DR_B)
    kv = (rms_norm(ckv, ckv_g) @ w_ukv).reshape(B, S, H_B, DN_B + DV_B)
    k_nope = kv[..., :DN_B]
    v = kv[..., DN_B:]
    k = jnp.concatenate([k_nope, jnp.broadcast_to(kr[:, :, None, :], (B, S, H_B, DR_B))], axis=-1)
    q = rms_norm(q, q_g)
    k = rms_norm(k, k_g)
    q = jnp.concatenate([q[..., :DN_B], rope(q[..., DN_B:], pos)], axis=-1)
    k = jnp.concatenate([k[..., :DN_B], rope(k[..., DN_B:], pos)], axis=-1)
    o = dense_attention(q[:, :, :, None], k, v, None, (DN_B + DR_B) ** -0.5)
    return o.reshape(B, S, H_B * DV_B)


def axial_gqa(cq, ck, cv, q_g, k_g, row, col):
    B, S, _ = cq.shape
    q = axial_rope(rms_norm(cq.reshape(B, S, H_C, DH_C), q_g), row, col)
    k = axial_rope(rms_norm(ck.reshape(B, S, KV_C, DH_C), k_g), row, col)
    v = cv.reshape(B, S, KV_C, DH_C)
    o = dense_attention(q.reshape(B, S, KV_C, H_C // KV_C, DH_C), k, v, None, DH_C ** -0.5)
    return o.reshape(B, S, H_C * DH_C)


def dilated_mixture(dq, dk, dv, q_g, k_g, bias_tbl):
    B, S, _ = dq.shape
    q = rms_norm(dq.reshape(B, S, N_DIL, H_D, DH_D), q_g)
    k = rms_norm(dk.reshape(B, S, H_D, DH_D), k_g)
    v = dv.reshape(B, S, H_D, DH_D)
    outs = []
    lses = []
    for g, (window, dil) in enumerate(DILATIONS):
        o, l = dilated_band_attention(q[:, :, g], k, v, window, dil,
                                      bias_tbl[:, g * H_D:(g + 1) * H_D], DH_D ** -0.5)
        outs.append(o)
        lses.append(l)
    w = jax.nn.softmax(jnp.stack(lses), axis=0)
    o = jnp.einsum('gbsh,gbshd->bshd', w.astype(v.dtype), jnp.stack(outs))
    return o.reshape(B, S, H_D * DH_D)


def setup_inputs(seed: int = 0) -> dict:
    key = jax.random.key(seed)
    ks = iter(jax.random.split(key, 40))
    f32 = jnp.float32

    def nrm(shape, scale):
        return jax.random.normal(next(ks), shape, f32) * scale

    def gain(shape):
        return 1.0 + 0.1 * jax.random.normal(next(ks), shape, f32)

    return {
        'x_prompt': nrm((BATCH, SEQ, D_MODEL), 1.0),
        'x_sample': nrm((DEC_BATCH, DEC_SEQ, D_MODEL), 1.0),
        'p_prompt': nrm((DEPTH, BATCH, SEQ, PLE_DIM), 1.0),
        'p_sample': nrm((DEPTH, DEC_BATCH, DEC_SEQ, PLE_DIM), 1.0),
        'rel_bias': nrm((N_BUCKETS, N_BIAS_HEADS), 0.5),
        'norm_mix': gain((DEPTH, D_MODEL)),
        'w_in': nrm((DEPTH, D_MODEL, IN_WIDTH), D_MODEL ** -0.5),
        'a_q_norm': gain((DEPTH, DK_A)),
        'a_k_norm': gain((DEPTH, DK_A)),
        'a_lambda_q1': nrm((DEPTH, DK_A), 0.1),
        'a_lambda_k1': nrm((DEPTH, DK_A), 0.1),
        'a_lambda_q2': nrm((DEPTH, DK_A), 0.1),
        'a_lambda_k2': nrm((DEPTH, DK_A), 0.1),
        'a_out_norm': gain((DEPTH, DV_A)),
        'b_cq_norm': gain((DEPTH, Q_RANK_B)),
        'b_ckv_norm': gain((DEPTH, KV_RANK_B)),
        'b_w_uq': nrm((DEPTH, Q_RANK_B, H_B * (DN_B + DR_B)), Q_RANK_B ** -0.5),
        'b_w_ukv': nrm((DEPTH, KV_RANK_B, H_B * (DN_B + DV_B)), KV_RANK_B ** -0.5),
        'b_q_norm': gain((DEPTH, DN_B + DR_B)),
        'b_k_norm': gain((DEPTH, DN_B + DR_B)),
        'c_q_norm': gain((DEPTH, DH_C)),
        'c_k_norm': gain((DEPTH, DH_C)),
        'd_q_norm': gain((DEPTH, DH_D)),
        'd_k_norm': gain((DEPTH, DH_D)),
        'w_gate': nrm((DEPTH, N_BRANCH, D_MODEL, D_MODEL), D_MODEL ** -0.5),
        'w_branch': nrm((DEPTH, N_BRANCH, BRANCH_W, D_MODEL), BRANCH_W ** -0.5),
        'w_out': nrm((DEPTH, D_MODEL, D_MODEL), D_MODEL ** -0.5),
        'norm_ffn': gain((DEPTH, D_MODEL)),
        'w_ff1': nrm((DEPTH, D_MODEL, D_FF), D_MODEL ** -0.5),
        'w_ff2': nrm((DEPTH, D_FF, D_MODEL), D_FF ** -0.5),
        'norm_ple': gain((DEPTH, D_MODEL)),
        'w_ple_gate': nrm((DEPTH, D_MODEL, D_MODEL), D_MODEL ** -0.5),
        'w_ple_proj': nrm((DEPTH, PLE_DIM, D_MODEL), PLE_DIM ** -0.5),
    }


def reference(x_prompt, x_sample, p_prompt, p_sample, rel_bias, norm_mix, w_in,
              a_q_norm, a_k_norm, a_lambda_q1, a_lambda_k1, a_lambda_q2, a_lambda_k2, a_out_norm,
              b_cq_norm, b_ckv_norm, b_w_uq, b_w_ukv, b_q_norm, b_k_norm,
              c_q_norm, c_k_norm, d_q_norm, d_k_norm,
              w_gate, w_branch, w_out, norm_ffn, w_ff1, w_ff2,
              norm_ple, w_ple_gate, w_ple_proj):
    bias_a = rel_bias[:, :H_A]
    bias_d = rel_bias[:, H_A:]

    def trunk(x, p):
        B, S, _ = x.shape
        rows = S // GRID_W
        pos = jnp.arange(S, dtype=jnp.float32)
        row = jnp.repeat(jnp.arange(rows, dtype=jnp.float32), GRID_W)
        col = jnp.tile(jnp.arange(GRID_W, dtype=jnp.float32), rows)
        for i in range(DEPTH):
            lam_init = 0.8 - 0.6 * math.exp(-0.3 * i)
            h = rms_norm(x, norm_mix[i])
            (aq, ak, av, bcq, bckv, bkr, cq, ck, cv, dq, dk, dv) = jnp.split(h @ w_in[i], IN_SPLITS, axis=-1)
            branches = (
                diff_attention(aq, ak, av, a_q_norm[i], a_k_norm[i], a_lambda_q1[i], a_lambda_k1[i],
                               a_lambda_q2[i], a_lambda_k2[i], a_out_norm[i], bias_a, lam_init),
                latent_attention(bcq, bckv, bkr, b_cq_norm[i], b_ckv_norm[i], b_w_uq[i], b_w_ukv[i],
                                 b_q_norm[i], b_k_norm[i], pos),
                axial_gqa(cq, ck, cv, c_q_norm[i], c_k_norm[i], row, col),
                dilated_mixture(dq, dk, dv, d_q_norm[i], d_k_norm[i], bias_d),
            )
            merged = jax.nn.sigmoid(h @ w_gate[i, 0]) * (branches[0] @ w_branch[i, 0])
            for b in range(1, N_BRANCH):
                merged = merged + jax.nn.sigmoid(h @ w_gate[i, b]) * (branches[b] @ w_branch[i, b])
            x = x + merged @ w_out[i]
            h2 = rms_norm(x, norm_ffn[i])
            x = x + jnp.square(jax.nn.relu(h2 @ w_ff1[i])) @ w_ff2[i]
            h3 = rms_norm(x, norm_ple[i])
            x = x + jax.nn.sigmoid(h3 @ w_ple_gate[i]) * (p[i] @ w_ple_proj[i])
        return x

    y_prompt = trunk(x_prompt, p_prompt)
    y_sample = trunk(x_sample, p_sample)
    return (y_prompt, y_sample)
```

```python
import math
from contextlib import ExitStack
import numpy as np
import concourse.bass as bass
import concourse.mybir as mybir
from concourse.bass_utils import run_bass_kernel_spmd

F32 = mybir.dt.float32
BF16 = mybir.dt.bfloat16
ALU = mybir.AluOpType
AF = mybir.ActivationFunctionType

D_MODEL = 2048
NCH = 16
T = 512
PIECE = 1024
EPS = 1e-6
NEG = -30000.0
N_BUCKETS = 32
REL_MAX_DIST = 1024
ROPE_THETA = 10000.0
GRID_W = 64
DILS = ((128, 1), (512, 4), (2048, 16))
TC = 1535
TW = 2944
TG = TW + 128
KVT = 32
ENGS = ("pe", "act", "dve", "pool", "sp")


class Reg:
    __slots__ = ("name", "lw", "rd")

    def __init__(self, name=""):
        self.name = name
        self.lw = None
        self.rd = {}


class Plan:
    def __init__(self, ndma_sems=12):
        self.ops = {e: [] for e in ENGS}
        self.tick = {e: 0 for e in ENGS}
        self.known = {e: {} for e in ENGS}
        self.ndma = ndma_sems
        self.dma_cnt = {e: 0 for e in ENGS}
        self.dma_use = {}
        self.final_waits = []

    def _deps(self, reads, writes):
        deps = {}
        for r in reads:
            if r.lw is not None:
                k, v = r.lw
                if deps.get(k, 0) < v:
                    deps[k] = v
        for w in writes:
            if w.lw is not None:
                k, v = w.lw
                if deps.get(k, 0) < v:
                    deps[k] = v
            for k, v in w.rd.items():
                if deps.get(k, 0) < v:
                    deps[k] = v
        return deps

    def _waits(self, eng, deps, skip_self=False):
        waits = []
        kn = self.known[eng]
        for k, v in deps.items():
            if skip_self and k == eng:
                continue
            if kn.get(k, 0) >= v:
                continue
            kn[k] = v
            waits.append((k, v))
        return waits

    def _mark(self, tok, reads, writes):
        k, v = tok
        for r in reads:
            if r.rd.get(k, 0) < v:
                r.rd[k] = v
        for w in writes:
            w.lw = tok
            w.rd = {}

    def op(self, eng, reads, writes, emit):
        deps = self._deps(reads, writes)
        waits = self._waits(eng, deps, skip_self=(eng == "pe"))
        self.tick[eng] += 1
        tok = (eng, self.tick[eng])
        self._mark(tok, reads, writes)
        self.ops[eng].append((waits, emit, (eng, 1)))

    def dma(self, eng, reads, writes, emit):
        deps = self._deps(reads, writes)
        n = self.dma_cnt[eng]
        self.dma_cnt[eng] += 1
        key = ("dma", eng, n % self.ndma)
        uses = self.dma_use.get(key, 0)
        if uses > 0 and deps.get(key, 0) < 16 * uses:
            deps[key] = 16 * uses
        self.dma_use[key] = uses + 1
        waits = self._waits(eng, deps)
        tok = (key, 16 * (uses + 1))
        self._mark(tok, reads, writes)
        self.ops[eng].append((waits, emit, (key, 16)))

    def finish(self, eng, regs):
        deps = {}
        for r in regs:
            if r.lw is not None:
                k, v = r.lw
                deps[k] = max(deps.get(k, 0), v)
        self.final_waits.append((eng, self._waits(eng, deps)))

    def sem_keys(self):
        keys = list(ENGS)
        for e in ENGS:
            for j in range(min(self.ndma, self.dma_cnt[e])):
                keys.append(("dma", e, j))
        return keys

    def bodies(self, sems):
        engmap = {"pe": "tensor", "act": "scalar", "dve": "vector", "pool": "gpsimd", "sp": "sync"}
        finals = {}
        for eng, w in self.final_waits:
            finals.setdefault(eng, []).extend(w)

        def make(eng):
            def body(e):
                for waits, emit, (ikey, inc) in self.ops[eng]:
                    for k, v in waits:
                        e.wait_ge(sems[k], v)
                    ins = emit(e)
                    ins.then_inc(sems[ikey], inc)
                for k, v in finals.get(eng, []):
                    e.wait_ge(sems[k], v)
            return body
        return {engmap[e]: make(e) for e in ENGS}


W_IN_OFF = dict(aq=0, ak=512, av=1024, bcq=1536, bckv=2048, bkr=2560, cq=2624, ck=3136, cv=3392,
                dq=3648, dk=5184, dv=5696)


def slab_table():
    tab = {}
    off = 0

    def add(name, kc, ncol):
        nonlocal off
        tab[name] = (off, kc, ncol)
        off += kc * ncol
    for n in ("ak", "av", "bckv", "dk", "dv", "aq", "bcq", "cq", "dq0", "dq1", "dq2"):
        add(n, 16, 512)
    add("bkr", 16, 128)
    add("ck", 16, 256)
    add("cv", 16, 256)
    add("ukv", 4, 1024)
    add("uq", 4, 1024)
    for b in range(4):
        for mg in range(4):
            for hf in range(2):
                add(f"gate{b}_{mg}_{hf}", 16, 256)
    for mg in range(4):
        for b in range(4):
            add(f"br{b}_{mg}", 4, 512)
        add(f"out{mg}", 16, 512)
        add(f"pg{mg}", 16, 512)
        add(f"pp{mg}", 2, 512)
    for j in range(16):
        add(f"ff1_{j}", 16, 512)
    for jb in range(4):
        for mg in range(4):
            add(f"ff2_{jb}_{mg}", 16, 512)
    return tab, off


def _slab(W, cols, kc):
    sub = W[:, cols]
    return np.ascontiguousarray(sub.reshape(kc, 128, len(cols)).transpose(1, 0, 2))


def pack_weights(inp, l):
    tab, tot = slab_table()
    out = np.empty((128, tot), np.float32)

    def put(name, arr):
        off, kc, ncol = tab[name]
        out[:, off:off + kc * ncol] = arr.reshape(128, kc * ncol)
    w_in = inp["w_in"][l]
    ar = np.arange
    for n in ("ak", "av", "bckv", "dk", "dv", "aq", "bcq", "cq"):
        put(n, _slab(w_in, W_IN_OFF[n] + ar(512), 16))
    for g in range(3):
        put(f"dq{g}", _slab(w_in, W_IN_OFF["dq"] + g * 512 + ar(512), 16))
    bk_ = np.zeros((128, 16, 128), np.float32)
    bk_[:, :, :64] = _slab(w_in, W_IN_OFF["bkr"] + ar(64), 16)
    put("bkr", bk_)
    put("ck", _slab(w_in, W_IN_OFF["ck"] + ar(256), 16))
    put("cv", _slab(w_in, W_IN_OFF["cv"] + ar(256), 16))
    ukv = inp["b_w_ukv"][l]
    cols = np.concatenate([h * 256 + ar(128) for h in range(4)] + [h * 256 + 128 + ar(128) for h in range(4)])
    put("ukv", _slab(ukv, cols, 4))
    uq = inp["b_w_uq"][l]
    uq_ = np.zeros((128, 4, 1024), np.float32)
    uq_[:, :, :512] = _slab(uq, np.concatenate([h * 192 + ar(128) for h in range(4)]), 4)
    for h in range(4):
        uq_[:, :, 512 + h * 128: 512 + h * 128 + 64] = _slab(uq, h * 192 + 128 + ar(64), 4)
    put("uq", uq_)
    for b in range(4):
        for mg in range(4):
            for hf in range(2):
                put(f"gate{b}_{mg}_{hf}", _slab(inp["w_gate"][l, b], mg * 512 + hf * 256 + ar(256), 16))
    for mg in range(4):
        for b in range(4):
            put(f"br{b}_{mg}", _slab(inp["w_branch"][l, b], mg * 512 + ar(512), 4))
        put(f"pp{mg}", _slab(inp["w_ple_proj"][l], mg * 512 + ar(512), 2))
        put(f"out{mg}", _slab(inp["w_out"][l], mg * 512 + ar(512), 16))
        put(f"pg{mg}", _slab(inp["w_ple_gate"][l], mg * 512 + ar(512), 16))
    for j in range(16):
        put(f"ff1_{j}", _slab(inp["w_ff1"][l], j * 512 + ar(512), 16))
    for jb in range(4):
        for mg in range(4):
            put(f"ff2_{jb}_{mg}", _slab(inp["w_ff2"][l][jb * 2048:(jb + 1) * 2048], mg * 512 + ar(512), 16))
    return out


def gain_table():
    tab = {}
    off = 0

    def add(n, k):
        nonlocal off
        tab[n] = off
        off += k
    for n in ("norm_mix", "norm_ffn", "norm_ple"):
        add(n, 16)
    add("b_cq", 4)
    add("b_ckv", 4)
    for n in ("a_q", "a_k", "a_out", "b_qn", "b_qr", "b_kn", "b_kr", "c_q", "c_k", "d_q", "d_k"):
        add(n, 1)
    return tab, off


def pack_gains(inp, l):
    tab, tot = gain_table()
    g = np.zeros((128, tot), np.float32)

    def col(n, v):
        v = np.asarray(v, np.float32)
        k = v.size // 128
        g[:, tab[n]:tab[n] + k] = v.reshape(k, 128).T
    col("norm_mix", inp["norm_mix"][l])
    col("norm_ffn", inp["norm_ffn"][l])
    col("norm_ple", inp["norm_ple"][l])
    col("b_cq", inp["b_cq_norm"][l])
    col("b_ckv", inp["b_ckv_norm"][l])
    col("a_q", np.tile(inp["a_q_norm"][l], 2))
    col("a_k", np.tile(inp["a_k_norm"][l], 2))
    col("a_out", inp["a_out_norm"][l])
    col("b_qn", inp["b_q_norm"][l][:128])
    col("b_qr", np.tile(inp["b_q_norm"][l][128:], 2))
    col("b_kn", inp["b_k_norm"][l][:128])
    col("b_kr", np.tile(inp["b_k_norm"][l][128:], 2))
    col("c_q", inp["c_q_norm"][l])
    col("c_k", inp["c_k_norm"][l])
    col("d_q", inp["d_q_norm"][l])
    col("d_k", inp["d_k_norm"][l])
    return g


def rel_bucket_np(rel):
    half = N_BUCKETS // 2
    exact = half // 2
    n = np.abs(rel)
    nf = np.maximum(n, 1).astype(np.float32)
    large = exact + (np.log(nf / np.float32(exact)) / np.float32(math.log(REL_MAX_DIST / exact)) * (half - exact)).astype(np.int32)
    large = np.minimum(large, half - 1)
    return np.where(rel > 0, half, 0) + np.where(n < exact, n, large)


def const_tables(smax):
    n = np.arange(TG, dtype=np.int64) - TC
    bk = rel_bucket_np(n.astype(np.int32))
    oh = np.zeros((36, TG), np.float32)
    oh[bk, np.arange(TG)] = 1.0
    for g, (win, dil) in enumerate(DILS):
        band = (win // (2 * dil)) * dil
        ok = (np.abs(n) <= band) & (n % dil == 0)
        oh[33 + g] = np.where(ok, 0.0, NEG)
    half = 32
    inv = np.power(np.float32(ROPE_THETA), (-2.0 * np.arange(half, dtype=np.float32) / 64).astype(np.float32)).astype(np.float32)
    pos = np.arange(smax, dtype=np.float32)
    row = np.floor(pos / GRID_W).astype(np.float32)
    colp = (pos - row * GRID_W).astype(np.float32)
    idx = np.arange(64) % 32

    def cs(p):
        ang = (p[None, :] * inv[idx][:, None]).astype(np.float32)
        return np.cos(ang).astype(np.float32), np.sin(ang).astype(np.float32)
    cb, sb = cs(pos)
    cr, sr = cs(row)
    cc, sc = cs(colp)
    rope = np.zeros((4, 128, smax), np.float32)
    rope[0, :64] = cb
    rope[1, :64] = sb
    rope[2] = np.concatenate([cr, cc], 0)
    rope[3] = np.concatenate([sr, sc], 0)
    def rot64():
        R = np.zeros((64, 64), np.float32)
        for m in range(32):
            R[m, m + 32] = -1.0
            R[m + 32, m] = 1.0
        return R
    RB = np.zeros((128, 128), np.float32)
    RB[:64, :64] = rot64().T
    RC = np.zeros((128, 128), np.float32)
    RC[:64, :64] = rot64().T
    RC[64:, 64:] = rot64().T
    bd = np.zeros((128, 128), np.float32)
    bd[:64, :64] = 1.0
    bd[64:, 64:] = 1.0
    mats = np.stack([np.ones((128, 128), np.float32), bd, RB, RC], 0)
    return oh, rope, mats


def build_program(seqs, depth):
    nc = bass.Bass("TRN2", target_bir_lowering=False)
    stab, STOT = slab_table()
    gtab, GTOT = gain_table()
    smax = max(seqs)
    P = Plan()
    es = ExitStack()

    def din(name, shape, dt=F32):
        return nc.dram_tensor(name, list(shape), dt, kind="ExternalInput").ap()

    def dint(name, shape, dt):
        return nc.dram_tensor(name, list(shape), dt, kind="Internal").ap()

    w32 = [din(f"w32_{l}", (128, STOT)) for l in range(depth)]
    w16 = [dint(f"w16_{l}", (128, STOT), BF16) for l in range(depth)]
    gains_d = din("gains", (128, depth * GTOT))
    lamv_d = din("lamv", (128, depth * 256))
    bcol_d = din("bcol", (128, 8))
    biasl_d = din("biasl", (36, 16))
    oh_d = din("oh", (36, TG))
    rope_d = din("rope", (4, 128, smax))
    mats_d = din("mats", (4, 128, 128))
    gs_d = dint("gs", (16, TG), F32)
    xin = [din(f"x{i}", (D_MODEL, S)) for i, S in enumerate(seqs)]
    pin = [din(f"p{i}", (depth, 256, S)) for i, S in enumerate(seqs)]
    xmid = [dint(f"xm{i}", (D_MODEL, S), F32) for i, S in enumerate(seqs)]
    yout = [nc.dram_tensor(f"y{i}", [D_MODEL, S], F32, kind="ExternalOutput").ap() for i, S in enumerate(seqs)]
    kvd = [dint(f"kv{i}", (S // PIECE, 128, KVT * 1024), BF16) for i, S in enumerate(seqs)]

    with es:
        def sb(name, shape, dt):
            return es.enter_context(nc.sbuf_tensor(name, list(shape), dt))
        XT = sb("XT", (128, NCH, T), F32)
        HB = sb("HB", (128, NCH, T), BF16)
        OB = sb("OB", (128, NCH, T), BF16)
        MG = sb("MG", (128, NCH, T), BF16)
        NSLOT = 2
        WS = sb("WS", (128, NSLOT, 8192), BF16)
        QB = sb("QB", (128, 4, T), BF16)
        QR = sb("QR", (128, 4, T), BF16)
        CQ = sb("CQ", (128, 4, T), BF16)
        KT = sb("KT", (128, 2, 2, 1024), BF16)
        VT = sb("VT", (128, 2, 1024), BF16)
        TB = sb("TB", (128, 3, TW), BF16)
        NPT = 4
        PT = sb("PT", (128, NPT, T), BF16)
        NSB = 2
        SBT = sb("SBT", (128, NSB, T), F32)
        NTMP = 8
        TMP = sb("TMP", (128, NTMP, T), F32)
        ROPE = sb("ROPE", (128, 4, T), F32)
        LL = sb("LL", (128, 2, T), F32)
        LTMP = sb("LTMP", (128, 64), F32)
        PLB = sb("PLB", (128, 2, T), BF16)
        KST = sb("KST", (128, 2, T), BF16)
        VST = sb("VST", (128, 2, 512), BF16)
        MATS = sb("MATS", (128, 4, 128), F32)
        ONESB = sb("ONESB", (128, 128), BF16)
        GV = sb("GV", (128, depth * GTOT), F32)
        LAMV = sb("LAMV", (128, depth * 256), F32)
        LAMT = sb("LAMT", (128, 8), F32)
        LAM = sb("LAM", (128, depth), F32)
        BCOL = sb("BCOL", (128, 8), F32)
        BIASL = sb("BIASL", (36, 16), F32)
        OHS = TMP[:].rearrange("p a t -> p (a t)")[0:36, 0:TG]
        GSB = XT[:].rearrange("p a t -> p (a t)")[0:16, 0:TG]
        PS = es.enter_context(nc.psum_tensor("PS", [128, 8 * 512], F32))

        def bank(i):
            return PS[:, i * 512:(i + 1) * 512]
        r_bank = [Reg(f"bank{i}") for i in range(8)]
        r_xt = [Reg() for _ in range(NCH)]
        r_hb = [Reg() for _ in range(NCH)]
        r_ob = [Reg() for _ in range(NCH)]
        r_mg = [Reg() for _ in range(NCH)]
        r_ws = [Reg() for _ in range(NSLOT)]
        r_qb = [Reg() for _ in range(4)]
        r_qr = [Reg() for _ in range(4)]
        r_cq = [Reg() for _ in range(4)]
        r_kt = [[Reg(), Reg()] for _ in range(2)]
        r_vt = [Reg() for _ in range(2)]
        r_tb = [Reg() for _ in range(3)]
        r_pt = [Reg() for _ in range(NPT)]
        r_sbt = [Reg() for _ in range(NSB)]
        r_tmp = [Reg() for _ in range(NTMP)]
        r_rope = Reg()
        r_plb = Reg()
        r_ll = [Reg(), Reg()]
        r_ltmp = Reg()
        r_kst = [Reg(), Reg()]
        r_vst = [Reg(), Reg()]
        r_const = Reg()
        r_gs = Reg()
        r_w16 = [{n: Reg() for n in stab} for _ in range(depth)]
        r_xm = [[Reg() for _ in range(S // T)] for S in seqs]
        r_y = [[Reg() for _ in range(S // T)] for S in seqs]
        r_kv = [[[Reg() for _ in range(KVT)] for _ in range(S // PIECE)] for S in seqs]

        ctr = dict(lin=0, aux=0, tmp=0, pt=0, sbt=0, ws=0, kst=0, vst=0)

        def lin_bank():
            ctr["lin"] = (ctr["lin"] + 1) % 4
            return ctr["lin"]

        def aux_bank():
            ctr["aux"] = (ctr["aux"] + 1) % 4
            return 4 + ctr["aux"]

        def tmp():
            ctr["tmp"] = (ctr["tmp"] + 1) % NTMP
            return ctr["tmp"]

        ONES32 = MATS[:, 0, :]
        BD64 = MATS[:, 1, :]
        RBm = MATS[:, 2, :]
        RCm = MATS[:, 3, :]

        def ld(dst, src, reg, eng="pool"):
            P.dma(eng, [], [reg], lambda e: e.dma_start(out=dst, in_=src))
        ld(MATS[:], mats_d.rearrange("m p f -> p m f"), r_const)
        ld(GV[:], gains_d, r_const)
        ld(LAMV[:], lamv_d, r_const)
        ld(BCOL[:], bcol_d, r_const)
        ld(BIASL[:], biasl_d, r_const)
        P.op("dve", [], [r_const], lambda e: e.memset(ONESB[:], 1.0))
        for l in range(depth):
            lam_init = 0.8 - 0.6 * math.exp(-0.3 * l)
            base = l * 256
            for j in range(2):
                a = LAMV[:, base + j * 128: base + j * 128 + 64]
                b = LAMV[:, base + j * 128 + 64: base + j * 128 + 128]
                t0 = LTMP[:, :]
                dst = LAMT[:, 2 * j:2 * j + 1]
                P.op("dve", [r_const], [r_ltmp], lambda e, a=a, b=b, t0=t0: e.tensor_tensor(out=t0, in0=a, in1=b, op=ALU.mult))
                P.op("dve", [r_ltmp], [r_const], lambda e, t0=t0, dst=dst: e.reduce_sum(out=dst, in_=t0, axis=mybir.AxisListType.X))
                P.op("act", [r_const], [r_const], lambda e, dst=dst: e.activation(out=dst, in_=dst, func=AF.Exp))
            P.op("dve", [r_const], [r_const], lambda e: e.tensor_tensor(out=LAMT[:, 4:5], in0=LAMT[:, 2:3], in1=LAMT[:, 0:1], op=ALU.subtract))
            P.op("dve", [r_const], [r_const], lambda e, l=l, li=lam_init: e.tensor_scalar(out=LAM[:, l:l + 1], in0=LAMT[:, 4:5], scalar1=-li, scalar2=0.0, op0=ALU.add, op1=ALU.add))
        P.dma("pool", [], r_tmp, lambda e: e.dma_start(out=OHS, in_=oh_d))
        for c in range(TG // 512):
            bk = lin_bank()
            P.op("pe", [r_const] + r_tmp, [r_bank[bk]], lambda e, bk=bk, c=c: e.matmul(bank(bk)[0:16, :], BIASL[:, :], OHS[:, c * 512:(c + 1) * 512], start=True, stop=True))
            P.op("act", [r_bank[bk]], r_xt[0:6], lambda e, bk=bk, c=c: e.activation(out=GSB[:, c * 512:(c + 1) * 512], in_=bank(bk)[0:16, :], func=AF.Copy))
        P.dma("pool", r_xt[0:6], [r_gs], lambda e: e.dma_start(out=gs_d, in_=GSB))
        for l in range(depth):
            for n, (off, kc, ncol) in stab.items():
                P.dma("pool", [], [r_w16[l][n]], lambda e, l=l, off=off, sz=kc * ncol: e.dma_start(out=w16[l][:, off:off + sz], in_=w32[l][:, off:off + sz]))

        def load_slab(l, name):
            off, kc, ncol = stab[name]
            s = ctr["ws"] = (ctr["ws"] + 1) % NSLOT
            P.dma("sp", [r_w16[l][name]], [r_ws[s]], lambda e: e.dma_start(out=WS[:, s, 0:kc * ncol], in_=w16[l][:, off:off + kc * ncol]))
            view = WS[:, s, 0:kc * ncol].rearrange("p (k n) -> p k n", k=kc)
            return view, r_ws[s]

        def gcol(l, name, j=0):
            c = l * GTOT + gtab[name] + j
            return GV[:, c:c + 1]

        def rstd_from(ssb, n, extra=1.0, rows=128):
            t = tmp()
            dst = TMP[0:rows, t, :]
            P.op("dve", [r_bank[ssb]], [r_tmp[t]], lambda e: e.tensor_scalar(out=dst, in0=bank(ssb)[0:rows, :], scalar1=1.0 / (n * extra * extra), scalar2=EPS / (extra * extra), op0=ALU.mult, op1=ALU.add))
            P.op("act", [r_tmp[t]], [r_tmp[t]], lambda e: e.activation(out=dst, in_=dst, func=AF.Sqrt))
            P.op("dve", [r_tmp[t]], [r_tmp[t]], lambda e: e.reciprocal(out=dst, in_=dst))
            return t

        def square_to_tmp(src, src_regs, rows=128):
            t = tmp()
            P.op("act", src_regs, [r_tmp[t]], lambda e: e.activation(out=TMP[0:rows, t, :], in_=src, func=AF.Square))
            return t

        def rmsnorm_x(l, gname):
            ssb = aux_bank()
            sq = []
            for c in range(NCH):
                t = square_to_tmp(XT[:, c, :], [r_xt[c]])
                P.op("pe", [r_const, r_tmp[t]], [r_bank[ssb]], lambda e, c=c, t=t: e.matmul(bank(ssb), ONES32, TMP[:, t, :], start=(c == 0), stop=(c == NCH - 1)))
            rt = rstd_from(ssb, D_MODEL)
            for c in range(NCH):
                P.op("dve", [r_xt[c], r_tmp[rt], r_const], [r_hb[c]], lambda e, c=c: e.scalar_tensor_tensor(out=HB[:, c, :], in0=XT[:, c, :], scalar=gcol(l, gname, c), in1=TMP[:, rt, :], op0=ALU.mult, op1=ALU.mult))

        def lin_fm(slab, sreg, col0, ncol, ins, in_regs, kcs=None, kc0=0):
            bk = lin_bank()
            n = len(ins)

            def emit(e):
                ins_ = None
                for i in range(n):
                    ins_ = e.matmul(bank(bk)[0:ncol, :], slab[:, kc0 + i, col0:col0 + ncol], ins[i], start=(i == 0), stop=(i == n - 1))
                return ins_
            P.op("pe", [sreg] + in_regs, [r_bank[bk]], emit)
            return bk

        def load_x(si, l, ti):
            src = xin[si] if l == 0 else xmid[si]
            rd = [] if l == 0 else [r_xm[si][ti]]
            for c in range(NCH):
                P.dma("pool", rd, [r_xt[c]], lambda e, c=c: e.dma_start(out=XT[:, c, :], in_=src[c * 128:(c + 1) * 128, ti * T:(ti + 1) * T]))

        def load_rope(ti):
            P.dma("pool", [], [r_rope], lambda e: e.dma_start(out=ROPE[:], in_=rope_d[:, :, ti * T:(ti + 1) * T].rearrange("m p t -> p m t")))

        def headnorm(bk, rows, ones_l, n, g_ap, dst, dst_regs, rope=None, extra_sq=None, keep32=False):
            t = square_to_tmp(bank(bk)[0:rows, :], [r_bank[bk]], rows)
            ssb = aux_bank()
            parts = [(ones_l[0:rows, 0:rows], TMP[0:rows, t, :])]
            regs = [r_const, r_tmp[t]]
            if extra_sq is not None:
                parts.append((extra_sq[0], extra_sq[1]))
                regs += extra_sq[2]

            def emit(e):
                ins = None
                for i, (a, b) in enumerate(parts):
                    ins = e.matmul(bank(ssb)[0:rows, :], a, b, start=(i == 0), stop=(i == len(parts) - 1))
                return ins
            P.op("pe", regs, [r_bank[ssb]], emit)
            rt = rstd_from(ssb, n, rows=rows)
            if rope is None:
                P.op("dve", [r_bank[bk], r_tmp[rt], r_const], dst_regs, lambda e: e.scalar_tensor_tensor(out=dst, in0=bank(bk)[0:rows, :], scalar=g_ap[0:rows, :], in1=TMP[0:rows, rt, :], op0=ALU.mult, op1=ALU.mult))
                return rt
            rot, cos_ap, sin_ap = rope
            tn = tmp()
            P.op("dve", [r_bank[bk], r_tmp[rt], r_const], [r_tmp[tn]], lambda e: e.scalar_tensor_tensor(out=TMP[0:rows, tn, :], in0=bank(bk)[0:rows, :], scalar=g_ap[0:rows, :], in1=TMP[0:rows, rt, :], op0=ALU.mult, op1=ALU.mult))
            apply_rope(tn, rows, rot, cos_ap, sin_ap, dst, dst_regs)
            return rt

        def apply_rope(tn, rows, rot, cos_ap, sin_ap, dst, dst_regs, scale_t=None):
            rb = aux_bank()
            P.op("pe", [r_const, r_tmp[tn]], [r_bank[rb]], lambda e: e.matmul(bank(rb)[0:rows, :], rot[0:rows, 0:rows], TMP[0:rows, tn, :], start=True, stop=True))
            t1 = tmp()
            P.op("dve", [r_tmp[tn], r_rope], [r_tmp[t1]], lambda e: e.tensor_tensor(out=TMP[0:rows, t1, :], in0=TMP[0:rows, tn, :], in1=cos_ap, op=ALU.mult))
            t2 = tmp()
            P.op("dve", [r_bank[rb], r_rope], [r_tmp[t2]], lambda e: e.tensor_tensor(out=TMP[0:rows, t2, :], in0=bank(rb)[0:rows, :], in1=sin_ap, op=ALU.mult))
            if scale_t is None:
                P.op("dve", [r_tmp[t1], r_tmp[t2]], dst_regs, lambda e: e.tensor_tensor(out=dst, in0=TMP[0:rows, t1, :], in1=TMP[0:rows, t2, :], op=ALU.add))
                return None
            P.op("dve", [r_tmp[t1], r_tmp[t2]], [r_tmp[t1]], lambda e: e.tensor_tensor(out=TMP[0:rows, t1, :], in0=TMP[0:rows, t1, :], in1=TMP[0:rows, t2, :], op=ALU.add))
            return t1

        def store_k(si, tok0, ktile, src_slot, rows=128):
            pc = tok0 // PIECE
            toff = tok0 % PIECE
            P.dma("pool", [r_kst[src_slot]], [r_kv[si][pc][ktile]], lambda e: e.dma_start(out=kvd[si][pc, 0:rows, ktile * 1024 + toff: ktile * 1024 + toff + T], in_=KST[0:rows, src_slot, :]))

        def kst_slot():
            ctr["kst"] = (ctr["kst"] + 1) % 2
            return ctr["kst"]

        def v_tokmajor(si, tok0, slab, sreg, col0, nheads, vt0, ins_tile, in_regs, nk):
            pc = tok0 // PIECE
            for tcn in range(T // 128):
                bk = lin_bank()
                ncol = nheads * 128

                def emit(e, bk=bk, tcn=tcn):
                    ins_ = None
                    for kc in range(nk):
                        ins_ = e.matmul(bank(bk)[:, 0:ncol], ins_tile[:, kc, tcn * 128:(tcn + 1) * 128], slab[:, kc, col0:col0 + ncol], start=(kc == 0), stop=(kc == nk - 1))
                    return ins_
                P.op("pe", [sreg] + in_regs, [r_bank[bk]], emit)
                vs = ctr["vst"] = (ctr["vst"] + 1) % 2
                P.op("act", [r_bank[bk]], [r_vst[vs]], lambda e, bk=bk, vs=vs: e.activation(out=VST[:, vs, 0:ncol], in_=bank(bk)[:, 0:ncol], func=AF.Copy))
                chunk = (tok0 % PIECE) // 128 + tcn
                dst = kvd[si][pc, :, vt0 * 1024: (vt0 + nheads) * 1024].rearrange("p (h f) -> p h f", h=nheads)[:, :, chunk * 128:(chunk + 1) * 128]
                P.dma("pool", [r_vst[vs]], [r_kv[si][pc][vt0 + h] for h in range(nheads)], lambda e, vs=vs, dst=dst: e.dma_start(out=dst, in_=VST[:, vs, 0:ncol].rearrange("p (h f) -> p h f", h=nheads)))

        hb_all = lambda: [HB[:, c, :] for c in range(NCH)]

        def phase1(si, l, ti):
            tok0 = ti * T
            load_x(si, l, ti)
            load_rope(ti)
            rmsnorm_x(l, "norm_mix")
            hins = hb_all()
            import os
            KSUB = int(os.environ.get("KSUB", "99"))
            if KSUB < 2:
                return
            for name, ones_l, n, gname, kt0 in (("ak", BD64, 64, "a_k", 0), ("dk", ONES32, 128, "d_k", 14)):
                slab, sreg = load_slab(l, name)
                for h in range(4):
                    bk = lin_fm(slab, sreg, h * 128, 128, hins, r_hb)
                    ks = kst_slot()
                    headnorm(bk, 128, ones_l, n, gcol(l, gname), KST[:, ks, :], [r_kst[ks]])
                    store_k(si, tok0, kt0 + h, ks)
            if KSUB < 3:
                return
            slab, sreg = load_slab(l, "ck")
            for h in range(2):
                bk = lin_fm(slab, sreg, h * 128, 128, hins, r_hb)
                ks = kst_slot()
                headnorm(bk, 128, ONES32, 128, gcol(l, "c_k"), KST[:, ks, :], [r_kst[ks]], rope=(RCm, ROPE[:, 2, :], ROPE[:, 3, :]))
                store_k(si, tok0, 12 + h, ks)
            if KSUB < 4:
                return
            slab, sreg = load_slab(l, "av")
            v_tokmajor(si, tok0, slab, sreg, 0, 4, 18, HB, r_hb, NCH)
            slab, sreg = load_slab(l, "dv")
            v_tokmajor(si, tok0, slab, sreg, 0, 4, 28, HB, r_hb, NCH)
            slab, sreg = load_slab(l, "cv")
            v_tokmajor(si, tok0, slab, sreg, 0, 2, 26, HB, r_hb, NCH)
            if KSUB < 5:
                return
            latent_norm(l, "bckv", "b_ckv", hins)
            if KSUB < 6:
                return
            slab, sreg = load_slab(l, "bkr")
            bkr = lin_fm(slab, sreg, 0, 128, hins, r_hb)
            P.op("act", [r_bank[bkr]], [r_ll[0]], lambda e: e.activation(out=LL[:, 0, :], in_=bank(bkr), func=AF.Square))
            tg = tmp()
            P.op("act", [r_bank[bkr], r_const], [r_tmp[tg]], lambda e: e.activation(out=TMP[:, tg, :], in_=bank(bkr), func=AF.Copy, scale=gcol(l, "b_kr")))
            apply_rope(tg, 128, RBm, ROPE[:, 0, :], ROPE[:, 1, :], LL[:, 1, :], [r_ll[1]])
            if KSUB < 7:
                return
            slab, sreg = load_slab(l, "ukv")
            cins = [CQ[:, c, :] for c in range(4)]
            for h in range(4):
                bk = lin_fm(slab, sreg, h * 128, 128, cins, r_cq)
                ks = kst_slot()
                rt = headnorm(bk, 128, ONES32, 192, gcol(l, "b_kn"), KST[:, ks, :], [r_kst[ks]],
                              extra_sq=(ONES32, LL[:, 0, :], [r_ll[0]]))
                store_k(si, tok0, 4 + h, ks)
                ks2 = kst_slot()
                P.op("dve", [r_ll[1], r_tmp[rt]], [r_kst[ks2]], lambda e, ks2=ks2, rt=rt: e.tensor_tensor(out=KST[:, ks2, :], in0=LL[:, 1, :], in1=TMP[:, rt, :], op=ALU.mult))
                store_k(si, tok0, 8 + h, ks2)
            v_tokmajor(si, tok0, slab, sreg, 512, 4, 22, CQ, r_cq, 4)

        def latent_norm(l, wname, gname, hins):
            slab, sreg = load_slab(l, wname)
            bks = []
            ssb = aux_bank()
            for c in range(4):
                bk = lin_fm(slab, sreg, c * 128, 128, hins, r_hb)
                bks.append(bk)
                t = square_to_tmp(bank(bk), [r_bank[bk]])
                P.op("pe", [r_const, r_tmp[t]], [r_bank[ssb]], lambda e, c=c, t=t: e.matmul(bank(ssb), ONES32, TMP[:, t, :], start=(c == 0), stop=(c == 3)))
            rt = rstd_from(ssb, 512)
            for c in range(4):
                P.op("dve", [r_bank[bks[c]], r_tmp[rt], r_const], [r_cq[c]], lambda e, c=c: e.scalar_tensor_tensor(out=CQ[:, c, :], in0=bank(bks[c]), scalar=gcol(l, gname, c), in1=TMP[:, rt, :], op0=ALU.mult, op1=ALU.mult))

        def load_kv(si, pc, slot, ktiles, vtile):
            for j, kt in enumerate(ktiles):
                rows = 128
                P.dma("pool", [r_kv[si][pc][kt]], [r_kt[slot][j]], lambda e, j=j, kt=kt, rows=rows: e.dma_start(out=KT[0:rows, slot, j, :], in_=kvd[si][pc, 0:rows, kt * 1024:(kt + 1) * 1024]))
            P.dma("pool", [r_kv[si][pc][vtile]], [r_vt[slot]], lambda e: e.dma_start(out=VT[:, slot, :], in_=kvd[si][pc, :, vtile * 1024:(vtile + 1) * 1024]))

        def load_tables(hts, slots=None):
            for g, ht in zip(slots if slots is not None else range(len(hts)), hts):
                src = bass.AP(tensor=gs_d.tensor, offset=ht * TG, ap=[[1, 128], [1, TW]])
                P.dma("pool", [r_gs], [r_tb[g]], lambda e, g=g, src=src: e.dma_start(out=TB[:, g, :], in_=src))

        def tbl_ap(g, o):
            a = TB[:, g, o + TC - 511:o + TC + 1]
            pa = a.ap
            return bass.AP(tensor=a.tensor, offset=a.offset + 511 * pa[-1][0], ap=[list(pa[0]), [-pa[-1][0], 512]])

        class Acc:
            def __init__(self, ob, db):
                self.ob, self.db, self.n = ob, db, 0

        def attn_units(units, scale):
            pending = []

            def flush_one():
                pending.pop(0)()
            for u in units:
                sbk = lin_bank()
                parts = u["s_parts"]

                def emit(e, parts=parts, sbk=sbk):
                    ins = None
                    for i, (a, b) in enumerate(parts):
                        ins = e.matmul(bank(sbk), a, b, start=(i == 0), stop=(i == len(parts) - 1))
                    return ins
                P.op("pe", u["s_regs"], [r_bank[sbk]], emit)
                pt = ctr["pt"] = (ctr["pt"] + 1) % NPT
                bias = u["bias"]
                if bias is None:
                    P.op("act", [r_bank[sbk]], [r_pt[pt]], lambda e, sbk=sbk, pt=pt: e.activation(out=PT[:, pt, :], in_=bank(sbk), func=AF.Exp, scale=scale))
                elif bias[0] == "col":
                    P.op("act", [r_bank[sbk], r_const], [r_pt[pt]], lambda e, sbk=sbk, pt=pt, b=bias[1]: e.activation(out=PT[:, pt, :], in_=bank(sbk), func=AF.Exp, scale=scale, bias=b))
                else:
                    st = ctr["sbt"] = (ctr["sbt"] + 1) % NSB
                    P.op("dve", [r_bank[sbk], r_tb[bias[1]]], [r_sbt[st]], lambda e, sbk=sbk, st=st, b=bias[2]: e.scalar_tensor_tensor(out=SBT[:, st, :], in0=bank(sbk), scalar=scale, in1=b, op0=ALU.mult, op1=ALU.add))
                    P.op("act", [r_sbt[st]], [r_pt[pt]], lambda e, st=st, pt=pt: e.activation(out=PT[:, pt, :], in_=SBT[:, st, :], func=AF.Exp))
                acc = u["acc"]
                first = (acc.n == 0)
                acc.n += 1
                last = u["last"]

                def pv(pt=pt, acc=acc, first=first, last=last, v_ap=u["v_ap"], v_reg=u["v_reg"]):
                    P.op("pe", [v_reg, r_pt[pt]], [r_bank[acc.ob]], lambda e: e.matmul(bank(acc.ob), v_ap, PT[:, pt, :], start=first, stop=last))
                    P.op("pe", [r_const, r_pt[pt]], [r_bank[acc.db]], lambda e: e.matmul(bank(acc.db), ONESB[:], PT[:, pt, :], start=first, stop=last))
                pending.append(pv)
                if len(pending) > 2:
                    flush_one()
            while pending:
                flush_one()

        def softmax_out(acc, dst, dst_regs):
            t = tmp()
            P.op("dve", [r_bank[acc.db]], [r_tmp[t]], lambda e: e.reciprocal(out=TMP[:, t, :], in_=bank(acc.db)))
            P.op("dve", [r_bank[acc.ob], r_tmp[t]], dst_regs, lambda e: e.tensor_tensor(out=dst, in0=bank(acc.ob), in1=TMP[:, t, :], op=ALU.mult))
            return t

        def mixer_A(si, l, ti, S):
            q0 = ti * T
            npc = S // PIECE
            lam_init = 0.8 - 0.6 * math.exp(-0.3 * l)
            for h in range(4):
                tslot = h % 3
                load_tables([h], [tslot])
                a1, a2 = Acc(4, 5), Acc(6, 7)
                for pc in range(npc):
                    slot = pc % 2
                    load_kv(si, pc, slot, [h], 18 + h)
                    units = []
                    for kc in range(8):
                        o = pc * PIECE + kc * 128 - q0
                        last = (pc == npc - 1 and kc == 7)
                        for m, acc in ((0, a1), (1, a2)):
                            lo = m * 64
                            if o >= 1070:
                                bias = ("col", BCOL[:, 4 + h:5 + h])
                            elif o <= -686:
                                bias = ("col", BCOL[:, h:h + 1])
                            else:
                                bias = ("tbl", tslot, tbl_ap(tslot, o))
                            units.append(dict(s_parts=[(KT[lo:lo + 64, slot, 0, kc * 128:(kc + 1) * 128], QB[lo:lo + 64, h, :])],
                                              s_regs=[r_kt[slot][0], r_qb[h]], bias=bias,
                                              v_ap=VT[:, slot, kc * 128:(kc + 1) * 128], v_reg=r_vt[slot], acc=acc, last=last))
                    attn_units(units, 64 ** -0.5)
                t1 = tmp()
                softmax_out(a1, TMP[:, t1, :], [r_tmp[t1]])
                t2 = tmp()
                softmax_out(a2, TMP[:, t2, :], [r_tmp[t2]])
                P.op("dve", [r_tmp[t1], r_tmp[t2], r_const], [r_tmp[t1]], lambda e, t1=t1, t2=t2: e.scalar_tensor_tensor(out=TMP[:, t1, :], in0=TMP[:, t2, :], scalar=LAM[:, l:l + 1], in1=TMP[:, t1, :], op0=ALU.mult, op1=ALU.add))
                ts = square_to_tmp(TMP[:, t1, :], [r_tmp[t1]])
                ssb = aux_bank()
                P.op("pe", [r_const, r_tmp[ts]], [r_bank[ssb]], lambda e, ts=ts, ssb=ssb: e.matmul(bank(ssb), ONES32, TMP[:, ts, :], start=True, stop=True))
                rt = rstd_from(ssb, 128, extra=(1.0 - lam_init))
                P.op("dve", [r_tmp[t1], r_tmp[rt], r_const], [r_ob[h]], lambda e, t1=t1, rt=rt, h=h: e.scalar_tensor_tensor(out=OB[:, h, :], in0=TMP[:, t1, :], scalar=gcol(l, "a_out"), in1=TMP[:, rt, :], op0=ALU.mult, op1=ALU.mult))

        def mixer_plain(si, ti, S, nheads, qtiles, ktile_of, vtile_of, scale, ob0, rope_q=None):
            npc = S // PIECE
            for h in range(nheads):
                acc = Acc(4 + 2 * (h % 2), 5 + 2 * (h % 2))
                kts = ktile_of(h)
                for pc in range(npc):
                    slot = pc % 2
                    load_kv(si, pc, slot, kts, vtile_of(h))
                    units = []
                    for kc in range(8):
                        last = (pc == npc - 1 and kc == 7)
                        parts = []
                        regs = []
                        for (qap, rows, j, qreg) in qtiles[h]:
                            parts.append((KT[0:rows, slot, j, kc * 128:(kc + 1) * 128], qap))
                            regs += [r_kt[slot][j], qreg]
                        units.append(dict(s_parts=parts, s_regs=regs, bias=None,
                                          v_ap=VT[:, slot, kc * 128:(kc + 1) * 128], v_reg=r_vt[slot], acc=acc, last=last))
                    attn_units(units, scale)
                softmax_out(acc, OB[:, ob0 + h, :], [r_ob[ob0 + h]])

        def mixer_D(si, l, ti, S, hins):
            q0 = ti * T
            npc = S // PIECE
            qbuf = (QB, QR, CQ)
            qreg = (r_qb, r_qr, r_cq)
            for g in range(3):
                slab, sreg = load_slab(l, f"dq{g}")
                for h in range(4):
                    bk = lin_fm(slab, sreg, h * 128, 128, hins, r_hb)
                    headnorm(bk, 128, ONES32, 128, gcol(l, "d_q"), qbuf[g][:, h, :], [qreg[g][h]])
            for h in range(4):
                load_tables([4 + g * 4 + h for g in range(3)])
                acc = Acc(4 + 2 * (h % 2), 5 + 2 * (h % 2))
                allu = []
                for pc in range(npc):
                    ulist = []
                    for kc in range(8):
                        o = pc * PIECE + kc * 128 - q0
                        for g, (win, dil) in enumerate(DILS):
                            band = (win // (2 * dil)) * dil
                            if -band - 127 <= o <= band + 511:
                                ulist.append((kc, g, o))
                    if ulist:
                        allu.append((pc, ulist))
                for pi, (pc, ulist) in enumerate(allu):
                    slot = pi % 2
                    load_kv(si, pc, slot, [14 + h], 28 + h)
                    units = []
                    for ui, (kc, g, o) in enumerate(ulist):
                        last = (pi == len(allu) - 1 and ui == len(ulist) - 1)
                        units.append(dict(s_parts=[(KT[:, slot, 0, kc * 128:(kc + 1) * 128], qbuf[g][:, h, :])],
                                          s_regs=[r_kt[slot][0], qreg[g][h]], bias=("tbl", g, tbl_ap(g, o)),
                                          v_ap=VT[:, slot, kc * 128:(kc + 1) * 128], v_reg=r_vt[slot], acc=acc, last=last))
                    attn_units(units, 128 ** -0.5)
                softmax_out(acc, OB[:, 12 + h, :], [r_ob[12 + h]])

        def phase2(si, l, ti, S, last_layer):
            tok0 = ti * T
            load_x(si, l, ti)
            load_rope(ti)
            rmsnorm_x(l, "norm_mix")
            hins = hb_all()
            slab, sreg = load_slab(l, "aq")
            for h in range(4):
                bk = lin_fm(slab, sreg, h * 128, 128, hins, r_hb)
                headnorm(bk, 128, BD64, 64, gcol(l, "a_q"), QB[:, h, :], [r_qb[h]])
            mixer_A(si, l, ti, S)
            latent_norm(l, "bcq", "b_cq", hins)
            slab, sreg = load_slab(l, "uq")
            cins = [CQ[:, c, :] for c in range(4)]
            for h in range(4):
                bkr_ = lin_fm(slab, sreg, 512 + h * 128, 128, cins, r_cq)
                tsq = square_to_tmp(bank(bkr_), [r_bank[bkr_]], 128)
                tg = tmp()
                P.op("act", [r_bank[bkr_], r_const], [r_tmp[tg]], lambda e, bkr_=bkr_, tg=tg: e.activation(out=TMP[:, tg, :], in_=bank(bkr_), func=AF.Copy, scale=gcol(l, "b_qr")))
                bk = lin_fm(slab, sreg, h * 128, 128, cins, r_cq)
                rt = headnorm(bk, 128, ONES32, 192, gcol(l, "b_qn"), QB[:, h, :], [r_qb[h]],
                              extra_sq=(ONES32, TMP[:, tsq, :], [r_tmp[tsq]]))
                tr = apply_rope(tg, 128, RBm, ROPE[:, 0, :], ROPE[:, 1, :], None, None, scale_t=rt)
                P.op("dve", [r_tmp[tr], r_tmp[rt]], [r_qr[h]], lambda e, tr=tr, rt=rt, h=h: e.tensor_tensor(out=QR[:, h, :], in0=TMP[:, tr, :], in1=TMP[:, rt, :], op=ALU.mult))
            mixer_plain(si, ti, S, 4,
                        [[(QB[:, h, :], 128, 0, r_qb[h]), (QR[:, h, :], 128, 1, r_qr[h])] for h in range(4)],
                        lambda h: [4 + h, 8 + h], lambda h: 22 + h, 192 ** -0.5, 4)
            slab, sreg = load_slab(l, "cq")
            for h in range(4):
                bk = lin_fm(slab, sreg, h * 128, 128, hins, r_hb)
                headnorm(bk, 128, ONES32, 128, gcol(l, "c_q"), QB[:, h, :], [r_qb[h]], rope=(RCm, ROPE[:, 2, :], ROPE[:, 3, :]))
            mixer_plain(si, ti, S, 4, [[(QB[:, h, :], 128, 0, r_qb[h])] for h in range(4)],
                        lambda h: [12 + h // 2], lambda h: 26 + h // 2, 128 ** -0.5, 8)
            mixer_D(si, l, ti, S, hins)
            for mg in range(4):
                for b in range(4):
                    for hf in range(2):
                        gsl, greg = load_slab(l, f"gate{b}_{mg}_{hf}")
                        for m2 in range(2):
                            m4 = hf * 2 + m2
                            gb = lin_fm(gsl, greg, m2 * 128, 128, hins, r_hb)
                            P.op("act", [r_bank[gb]], [r_xt[4 + m4]], lambda e, gb=gb, m4=m4: e.activation(out=XT[:, 4 + m4, :], in_=bank(gb), func=AF.Sigmoid))
                    bsl, breg = load_slab(l, f"br{b}_{mg}")
                    oins = [OB[:, b * 4 + kc, :] for kc in range(4)]
                    for m4 in range(4):
                        bb = lin_fm(bsl, breg, m4 * 128, 128, oins, r_ob[b * 4:b * 4 + 4])
                        if b == 0:
                            P.op("dve", [r_bank[bb], r_xt[4 + m4]], [r_xt[m4]], lambda e, bb=bb, m4=m4: e.tensor_tensor(out=XT[:, m4, :], in0=bank(bb), in1=XT[:, 4 + m4, :], op=ALU.mult))
                        else:
                            P.op("dve", [r_bank[bb], r_xt[4 + m4]], [r_xt[4 + m4]], lambda e, bb=bb, m4=m4: e.tensor_tensor(out=XT[:, 4 + m4, :], in0=bank(bb), in1=XT[:, 4 + m4, :], op=ALU.mult))
                            P.op("pool", [r_xt[4 + m4], r_xt[m4]], [r_xt[m4]], lambda e, m4=m4: e.tensor_tensor(out=XT[:, m4, :], in0=XT[:, m4, :], in1=XT[:, 4 + m4, :], op=ALU.add))
                for m4 in range(4):
                    m = mg * 4 + m4
                    P.op("act", [r_xt[m4]], [r_mg[m]], lambda e, m=m, m4=m4: e.activation(out=MG[:, m, :], in_=XT[:, m4, :], func=AF.Copy))
            load_x(si, l, ti)
            mins = [MG[:, c, :] for c in range(NCH)]
            for mg in range(4):
                sl, sreg = load_slab(l, f"out{mg}")
                for m4 in range(4):
                    m = mg * 4 + m4
                    bk = lin_fm(sl, sreg, m4 * 128, 128, mins, r_mg)
                    P.op("dve", [r_bank[bk], r_xt[m]], [r_xt[m]], lambda e, bk=bk, m=m: e.tensor_tensor(out=XT[:, m, :], in0=bank(bk), in1=XT[:, m, :], op=ALU.add))
            rmsnorm_x(l, "norm_ffn")
            for jb in range(4):
                for jj in range(4):
                    sl, sreg = load_slab(l, f"ff1_{jb * 4 + jj}")
                    for m4 in range(4):
                        c = jj * 4 + m4
                        bk = lin_fm(sl, sreg, m4 * 128, 128, hins, r_hb)
                        t = tmp()
                        P.op("act", [r_bank[bk]], [r_tmp[t]], lambda e, bk=bk, t=t: e.activation(out=TMP[:, t, :], in_=bank(bk), func=AF.Relu))
                        P.op("pool", [r_tmp[t]], [r_ob[c]], lambda e, t=t, c=c: e.tensor_tensor(out=OB[:, c, :], in0=TMP[:, t, :], in1=TMP[:, t, :], op=ALU.mult))
                uins = [OB[:, c, :] for c in range(NCH)]
                for mg in range(4):
                    sl, sreg = load_slab(l, f"ff2_{jb}_{mg}")
                    for m4 in range(4):
                        m = mg * 4 + m4
                        bk = lin_fm(sl, sreg, m4 * 128, 128, uins, r_ob)
                        P.op("dve", [r_bank[bk], r_xt[m]], [r_xt[m]], lambda e, bk=bk, m=m: e.tensor_tensor(out=XT[:, m, :], in0=bank(bk), in1=XT[:, m, :], op=ALU.add))
            rmsnorm_x(l, "norm_ple")
            for j in range(2):
                t = tmp()
                P.dma("pool", [], [r_tmp[t]], lambda e, j=j, t=t: e.dma_start(out=TMP[:, t, :], in_=pin[si][l, j * 128:(j + 1) * 128, tok0:tok0 + T]))
                P.op("pool", [r_tmp[t]], [r_plb], lambda e, j=j, t=t: e.tensor_copy(out=PLB[:, j, :], in_=TMP[:, t, :]))
            pins = [PLB[:, j, :] for j in range(2)]
            for mg in range(4):
                gsl, greg = load_slab(l, f"pg{mg}")
                psl, preg = load_slab(l, f"pp{mg}")
                for m4 in range(4):
                    m = mg * 4 + m4
                    gb = lin_fm(gsl, greg, m4 * 128, 128, hins, r_hb)
                    tg = tmp()
                    P.op("act", [r_bank[gb]], [r_tmp[tg]], lambda e, gb=gb, tg=tg: e.activation(out=TMP[:, tg, :], in_=bank(gb), func=AF.Sigmoid))
                    pb = lin_fm(psl, preg, m4 * 128, 128, pins, [r_plb])
                    P.op("dve", [r_bank[pb], r_tmp[tg]], [r_tmp[tg]], lambda e, pb=pb, tg=tg: e.tensor_tensor(out=TMP[:, tg, :], in0=bank(pb), in1=TMP[:, tg, :], op=ALU.mult))
                    P.op("pool", [r_tmp[tg], r_xt[m]], [r_xt[m]], lambda e, tg=tg, m=m: e.tensor_tensor(out=XT[:, m, :], in0=XT[:, m, :], in1=TMP[:, tg, :], op=ALU.add))
            dst = yout[si] if last_layer else xmid[si]
            dreg = r_y[si][ti] if last_layer else r_xm[si][ti]
            for c in range(NCH):
                P.dma("pool", [r_xt[c]], [dreg], lambda e, c=c: e.dma_start(out=dst[c * 128:(c + 1) * 128, tok0:tok0 + T], in_=XT[:, c, :]))

        import os
        KDBG = int(os.environ.get("KDBG", "99"))
        for l in range(depth):
            for si, S in enumerate(seqs):
                if KDBG < 1:
                    continue
                for ti in range(S // T):
                    phase1(si, l, ti)
                    if KDBG < 2:
                        break
                if KDBG < 3:
                    continue
                for ti in range(S // T):
                    phase2(si, l, ti, S, l == depth - 1)
                    if KDBG < 4:
                        break
        P.finish("pool", [r for si in range(len(seqs)) for r in r_y[si]] + [r_gs] + [r for l in range(depth) for r in r_w16[l].values()] + [r for si in range(len(seqs)) for pc in r_kv[si] for r in pc])

        sems = {}
        for i, k in enumerate(P.sem_keys()):
            sems[k] = es.enter_context(nc.semaphore(f"s{i}"))
        with nc.Block() as block:
            bodies = P.bodies(sems)
            block.tensor(bodies["tensor"])
            block.scalar(bodies["scalar"])
            block.vector(bodies["vector"])
            block.gpsimd(bodies["gpsimd"])
            block.sync(bodies["sync"])
    return nc, P


def make_core_inputs(inp, depth, seq_x, seq_p, smax, shared=None):
    if shared is None:
        shared = {}
        for l in range(depth):
            shared[f"w32_{l}"] = pack_weights(inp, l)
        shared["gains"] = np.concatenate([pack_gains(inp, l) for l in range(depth)], axis=1)
        lam = []
        for l in range(depth):
            v = np.concatenate([inp["a_lambda_q1"][l], inp["a_lambda_k1"][l], inp["a_lambda_q2"][l], inp["a_lambda_k2"][l]])
            lam.append(np.broadcast_to(v[None, :], (128, 256)))
        shared["lamv"] = np.ascontiguousarray(np.concatenate(lam, axis=1), dtype=np.float32)
        rb = np.asarray(inp["rel_bias"], np.float32)
        shared["bcol"] = np.ascontiguousarray(np.broadcast_to(np.concatenate([rb[15, :4], rb[31, :4]])[None, :], (128, 8)), dtype=np.float32)
        biasl = np.zeros((36, 16), np.float32)
        biasl[:32] = rb
        for g in range(3):
            biasl[33 + g, 4 + 4 * g: 8 + 4 * g] = 1.0
        shared["biasl"] = biasl
        oh, rope, mats = const_tables(smax)
        shared["oh"] = oh
        shared["rope"] = rope
        shared["mats"] = mats
    m = dict(shared)
    for i, (x, p) in enumerate(zip(seq_x, seq_p)):
        m[f"x{i}"] = np.ascontiguousarray(np.asarray(x, np.float32).T)
        m[f"p{i}"] = np.ascontiguousarray(np.asarray(p, np.float32).transpose(0, 2, 1))
    return m, shared


_CACHE = {}


def run_cores(inp, depth, per_core_x, per_core_p, seqs):
    key = (tuple(seqs), depth)
    if key not in _CACHE:
        _CACHE[key] = build_program(list(seqs), depth)[0]
    nc = _CACHE[key]
    smax = max(seqs)
    in_maps = []
    shared = None
    for xs, ps in zip(per_core_x, per_core_p):
        m, shared = make_core_inputs(inp, depth, xs, ps, smax, shared)
        in_maps.append(m)
    import os
    ncores = int(os.environ.get("KCORES", "8"))
    res = run_bass_kernel_spmd(nc, in_maps[:ncores], core_ids=list(range(ncores)))
    out = list(res.results)
    while len(out) < len(in_maps):
        out.append(out[0])
    return out


def kernel(**inputs):
    inp = {k: np.asarray(v) for k, v in inputs.items()}
    depth = inp["w_in"].shape[0]
    xp, xs = inp["x_prompt"], inp["x_sample"]
    pp, ps = inp["p_prompt"], inp["p_sample"]
    nb = xs.shape[0]
    per = nb // 8
    seqs = [xs.shape[1]] * per + [xp.shape[1]]
    pcx, pcp = [], []
    for c in range(8):
        pcx.append([xs[c * per + j] for j in range(per)] + [xp[0]])
        pcp.append([ps[:, c * per + j] for j in range(per)] + [pp[:, 0]])
    res = run_cores(inp, depth, pcx, pcp, seqs)
    y_s = np.empty(xs.shape, np.float32)
    for c in range(8):
        for j in range(per):
            y_s[c * per + j] = res[c][f"y{j}"].T
    y_p = np.ascontiguousarray(res[0][f"y{per}"].T)[None]
    return (y_p.astype(np.float32), y_s)
```
